# Optimizing a Trainium2 kernel written in Bass

```python
import jax, jax.numpy as jnp
from jax import lax
import numpy as np

D_MODEL = 1024
BATCH = 8
SEQ = 2048
DEPTH = 4

ATTN_HEADS = 8
HEAD_DIM = 64
ATTN_WIDTH = ATTN_HEADS * HEAD_DIM
CONV_WIDTH = D_MODEL - ATTN_WIDTH
CONV_KERNEL = 31
D_FF = 2816
FFN_KERNEL = 3
BLOCK_Q = 128
IN_WIDTH = 3 * ATTN_WIDTH + 2 * CONV_WIDTH
EPS = 1e-6

kernel_name = "hybrid_stickbreak_conformer_convffn"


def rms_norm(x, g):
    xf = x.astype(jnp.float32)
    y = xf * lax.rsqrt(jnp.mean(xf * xf, axis=-1, keepdims=True) + EPS)
    return (y * g.astype(jnp.float32)).astype(x.dtype)


def layer_norm(x, g, b):
    xf = x.astype(jnp.float32)
    mu = jnp.mean(xf, axis=-1, keepdims=True)
    xc = xf - mu
    var = jnp.mean(xc * xc, axis=-1, keepdims=True)
    y = xc * lax.rsqrt(var + EPS) * g.astype(jnp.float32) + b.astype(jnp.float32)
    return y.astype(x.dtype)


def causal_depthwise_conv(x, w, b):
    k_width, channels = w.shape
    y = lax.conv_general_dilated(
        x, w[:, None, :].astype(x.dtype),
        window_strides=(1,), padding=[(k_width - 1, 0)],
        dimension_numbers=("NWC", "WIO", "NWC"),
        feature_group_count=channels)
    return y + b.astype(x.dtype)


def stick_breaking_attention(q, k, v):
    seq = q.shape[1]
    scale = q.shape[-1] ** -0.5
    outs = []
    for start in range(0, seq, BLOCK_Q):
        end = min(start + BLOCK_Q, seq)
        qb = q[:, start:end].astype(jnp.float32)
        kb = k[:, :end].astype(jnp.float32)
        vb = v[:, :end].astype(jnp.float32)
        z = jnp.einsum("bqhd,bkhd->bhqk", qb, kb) * scale
        t_idx = start + jnp.arange(end - start)[:, None]
        s_idx = jnp.arange(end)[None, :]
        mask = s_idx < t_idx
        log_beta = jax.nn.log_sigmoid(z)
        log_one_minus = jnp.where(mask, jax.nn.log_sigmoid(-z), 0.0)
        tail = lax.cumsum(log_one_minus, axis=3, reverse=True) - log_one_minus
        weights = jnp.where(mask, jnp.exp(log_beta + tail), 0.0)
        outs.append(jnp.einsum("bhqk,bkhd->bqhd", weights, vb))
    return jnp.concatenate(outs, axis=1).astype(v.dtype)


def setup_inputs(seed: int = 0) -> dict:
    key = jax.random.key(seed)
    ks = jax.random.split(key, 16)
    f32 = jnp.float32
    out_scale = (2.0 * DEPTH) ** -0.5

    def nrm(k, shape, s):
        return jax.random.normal(k, shape, f32) * s

    return {
        "x": jax.random.normal(ks[0], (BATCH, SEQ, D_MODEL), f32),
        "norm1_g": 1.0 + nrm(ks[1], (DEPTH, D_MODEL), 0.02),
        "w_in": nrm(ks[2], (DEPTH, D_MODEL, IN_WIDTH), D_MODEL ** -0.5),
        "q_norm_g": 1.0 + nrm(ks[3], (DEPTH, HEAD_DIM), 0.02),
        "k_norm_g": 1.0 + nrm(ks[4], (DEPTH, HEAD_DIM), 0.02),
        "conv_dw_w": nrm(ks[5], (DEPTH, CONV_KERNEL, CONV_WIDTH), CONV_KERNEL ** -0.5),
        "conv_dw_b": nrm(ks[6], (DEPTH, CONV_WIDTH), 0.01),
        "conv_ln_g": 1.0 + nrm(ks[7], (DEPTH, CONV_WIDTH), 0.02),
        "conv_ln_b": nrm(ks[8], (DEPTH, CONV_WIDTH), 0.01),
        "w_out": nrm(ks[9], (DEPTH, D_MODEL, D_MODEL), D_MODEL ** -0.5 * out_scale),
        "norm2_g": 1.0 + nrm(ks[10], (DEPTH, D_MODEL), 0.02),
        "w_up": nrm(ks[11], (DEPTH, D_MODEL, 2 * D_FF), D_MODEL ** -0.5),
        "ffn_dw_w": nrm(ks[12], (DEPTH, FFN_KERNEL, D_FF), FFN_KERNEL ** -0.5),
        "ffn_dw_b": nrm(ks[13], (DEPTH, D_FF), 0.01),
        "w_down": nrm(ks[14], (DEPTH, D_FF, D_MODEL), D_FF ** -0.5 * out_scale),
    }


def reference(x, norm1_g, w_in, q_norm_g, k_norm_g, conv_dw_w, conv_dw_b,
              conv_ln_g, conv_ln_b, w_out, norm2_g, w_up, ffn_dw_w, ffn_dw_b,
              w_down):
    bsz, seq, _ = x.shape
    splits = [ATTN_WIDTH, 2 * ATTN_WIDTH, 3 * ATTN_WIDTH, 3 * ATTN_WIDTH + CONV_WIDTH]
    for layer in range(DEPTH):
        h = rms_norm(x, norm1_g[layer])
        proj = h @ w_in[layer]
        q, k, v, glu_a, glu_b = jnp.split(proj, splits, axis=-1)
        q = rms_norm(q.reshape(bsz, seq, ATTN_HEADS, HEAD_DIM), q_norm_g[layer])
        k = rms_norm(k.reshape(bsz, seq, ATTN_HEADS, HEAD_DIM), k_norm_g[layer])
        v = v.reshape(bsz, seq, ATTN_HEADS, HEAD_DIM)
        attn = stick_breaking_attention(q, k, v).reshape(bsz, seq, ATTN_WIDTH)

        c = glu_a * jax.nn.sigmoid(glu_b)
        c = causal_depthwise_conv(c, conv_dw_w[layer], conv_dw_b[layer])
        c = jax.nn.silu(layer_norm(c, conv_ln_g[layer], conv_ln_b[layer]))

        mixed = jnp.concatenate([attn, c], axis=-1) @ w_out[layer]
        x = x + mixed

        h = rms_norm(x, norm2_g[layer])
        gate, val = jnp.split(h @ w_up[layer], 2, axis=-1)
        gate = jax.nn.silu(causal_depthwise_conv(gate, ffn_dw_w[layer], ffn_dw_b[layer]))
        x = x + (gate * val) @ w_down[layer]
    return x
```

```python
import contextlib
import numpy as np
import concourse.bass as bass
import concourse.mybir as mybir
from concourse.bass_utils import run_bass_kernel_spmd

F32 = mybir.dt.float32
BF16 = mybir.dt.bfloat16
AF = mybir.ActivationFunctionType
ALU = mybir.AluOpType

S = 2048
D = 1024
DEPTH = 4
NH = 8
DFF = 2816
NJ = DFF // 128
EPS = 1e-6
NPAR = 242
N_CORES = 8

COMPUTE = ("pe", "act", "dve", "pool")
QUEUES = ("pe", "act", "dve", "pool", "sp")


class Buf:
    __slots__ = ("writer", "readers", "const")

    def __init__(self):
        self.writer = None
        self.readers = []
        self.const = False


class Op:
    __slots__ = ("eng", "fn", "pos", "waits", "sig", "vc", "is_dma", "sem", "val", "ndma", "sigval")

    def __init__(self, eng, fn):
        self.eng = eng
        self.fn = fn
        self.waits = []
        self.sig = False
        self.is_dma = False
        self.sem = None
        self.val = 0
        self.ndma = 0
        self.sigval = None


class Prog:
    def __init__(self, nc, same_engine_dist=2):
        self.nc = nc
        self.streams = {q: [] for q in QUEUES}
        self.clk = {q: {} for q in QUEUES}
        self.selfw = {q: -1 for q in QUEUES}
        self.dma_sems = {}
        self.eng_sems = {}
        self.same_engine_dist = same_engine_dist
        self.fence = {}

    def _need(self, o, d):
        if d.is_dma:
            return d.val > self.clk[o.eng].get(d.sem, 0)
        if d.eng == o.eng:
            if o.eng == "pe":
                return False
            if d.pos <= self.selfw[o.eng]:
                return False
            if o.is_dma:
                return True
            return (o.pos - d.pos) <= self.same_engine_dist
        return d.pos > self.clk[o.eng].get(d.eng, -1)

    def barrier(self):
        last = [s[-1] for q, s in self.streams.items() if s and q in COMPUTE]
        for q in QUEUES:
            self.fence[q] = list(last)

    def op(self, eng, fn, reads=(), writes=(), dma_sem=None, ndma=1):
        o = Op(eng, fn)
        o.pos = len(self.streams[eng])
        if dma_sem is not None:
            o.is_dma = True
            o.sem = dma_sem
            o.ndma = ndma
            st = self.dma_sems.setdefault(dma_sem, [None, 0])
            st[1] += 16 * ndma
            o.val = st[1]
        deps = []
        for b in reads:
            if b.writer is not None:
                deps.append(b.writer)
        for b in writes:
            if b.writer is not None:
                deps.append(b.writer)
            deps.extend(b.readers)
        f = self.fence.get(eng)
        if f:
            deps.extend(f)
            self.fence[eng] = None
        clk = self.clk[eng]
        seen = set()
        for d in deps:
            if d is o or id(d) in seen:
                continue
            seen.add(id(d))
            if self._need(o, d):
                o.waits.append(d)
                d.sig = True
                for k, v in d.vc.items():
                    if v > clk.get(k, -1):
                        clk[k] = v
                if d.eng == eng and not d.is_dma:
                    self.selfw[eng] = max(self.selfw[eng], d.pos)
        vc = dict(clk)
        if o.is_dma:
            vc[o.sem] = o.val
        else:
            vc[eng] = o.pos
        o.vc = vc
        for b in reads:
            if not b.const:
                b.readers.append(o)
        for b in writes:
            b.writer = o
            b.readers = []
        self.streams[eng].append(o)
        return o

    def emit(self, final_waits=()):
        nc = self.nc
        with contextlib.ExitStack() as ctx:
            for q in COMPUTE:
                self.eng_sems[q] = ctx.enter_context(nc.semaphore("s_" + q))
            for name, st in self.dma_sems.items():
                st[0] = ctx.enter_context(nc.semaphore("d_" + name))
            for d in final_waits:
                d.sig = True
            for q in COMPUTE:
                c = 0
                for o in self.streams[q]:
                    if o.sig and not o.is_dma:
                        c += 1
                        o.sigval = c
            block = ctx.enter_context(nc.Block())

            def run(q):
                def body(eng):
                    def wait(d):
                        if d.is_dma:
                            eng.wait_ge(self.dma_sems[d.sem][0], d.val)
                        else:
                            eng.wait_ge(self.eng_sems[d.eng], d.sigval)
                    for o in self.streams[q]:
                        for d in o.waits:
                            wait(d)
                        r = o.fn(eng)
                        if o.is_dma:
                            insts = r if isinstance(r, (list, tuple)) else [r]
                            assert len(insts) == o.ndma
                            for i in insts:
                                i.then_inc(self.dma_sems[o.sem][0], 16)
                        elif o.sig:
                            r.then_inc(self.eng_sems[q], 1)
                    if q == "sp":
                        for d in final_waits:
                            wait(d)
                return body

            block.tensor(run("pe"))
            block.scalar(run("act"))
            block.vector(run("dve"))
            block.gpsimd(run("pool"))
            block.sync(run("sp"))


class Rot:
    def __init__(self, items):
        self.items = list(items)
        self.i = 0

    def next(self):
        r = self.items[self.i % len(self.items)]
        self.i += 1
        return r


def build_program(n_layers=DEPTH, attn_debug=None, stop_after=None, dbg=None):
    nc = bass.Bass("TRN2", target_bir_lowering=False)
    xin = nc.dram_tensor("xin", [128, 8 * S], F32, kind="ExternalInput").ap()
    par_d = nc.dram_tensor("par", [n_layers, 128, NPAR], F32, kind="ExternalInput").ap()
    win_d = nc.dram_tensor("win", [n_layers * 5, 128, 4096], F32, kind="ExternalInput").ap()
    wout_d = nc.dram_tensor("wout", [n_layers * 2, 128, 4096], F32, kind="ExternalInput").ap()
    wup_d = nc.dram_tensor("wup", [n_layers * 11, 128, 4096], F32, kind="ExternalInput").ap()
    wdn_d = nc.dram_tensor("wdn", [n_layers * 8, 128, 2816], F32, kind="ExternalInput").ap()
    yout = nc.dram_tensor("yout", [128, 8 * S], F32, kind="ExternalOutput").ap()
    if dbg:
        dbg_h = nc.dram_tensor("dbg_h", [128, 8 * S], BF16, kind="ExternalOutput").ap()
        dbg_r = nc.dram_tensor("dbg_r", [128, 3 * 8192 + 4 * 2080], BF16, kind="ExternalOutput").ap()

    with contextlib.ExitStack() as ctx:
        def sb(name, shape, dt):
            return ctx.enter_context(nc.sbuf_tensor(name, shape, dt))

        xT = sb("xT", [128, 8 * S], F32)
        hT = sb("hT", [128, 8 * S], BF16)
        CST = 2080
        R = sb("R", [128, 3 * 8192 + 4 * CST], BF16)
        NS = 2
        wsl = [sb("wsl%d" % i, [128, 4096], BF16) for i in range(NS)]
        E = [sb("E%d" % i, [128, 1032], F32) for i in range(3)]
        ATT = sb("ATT", [128, 6144], BF16)
        sf = [E[0][:, 0:516], E[0][:, 516:1032], E[1][:, 0:516], E[1][:, 516:1032]]
        sbf = [ATT[:, 0:512], ATT[:, 512:1024]]
        diag = ATT[:, 2048:2048 + 31 * 128]
        SP = [ATT[:, 0:1024], ATT[:, 1024:2048]]
        SS = [ATT[:, 2048:3072], ATT[:, 3072:4096]]
        WW = [ATT[:, 4096:5120], ATT[:, 5120:6144]]
        ones = sb("ones", [128, 128], BF16)
        bdm = sb("bdm", [128, 128], BF16)
        tri = sb("tri", [128, 128], BF16)
        ident = sb("ident", [128, 128], BF16)
        maskw = sb("maskw", [128, 896], BF16)
        par = [sb("par%d" % i, [128, NPAR], F32) for i in range(2)]
        gqs = [sb("gqs%d" % i, [128, 1], F32) for i in range(2)]
        halo = sb("halo", [128, NJ * 2], F32)
        PSALL = ctx.enter_context(nc.psum_tensor("psall", [128, 4096], F32))
        ps = [PSALL[:, b * 512:(b + 1) * 512] for b in range(8)]

        p = Prog(nc)
        XB = [[Buf() for _ in range(4)] for _ in range(8)]
        HB = [[Buf() for _ in range(4)] for _ in range(8)]
        QB = [[Buf() for _ in range(4)] for _ in range(4)]
        KB = [[Buf() for _ in range(4)] for _ in range(4)]
        VB = [Buf() for _ in range(16)]
        C0B = [[Buf() for _ in range(4)] for _ in range(4)]
        UB = [[Buf() for _ in range(2)] for _ in range(NJ)]
        WB = [Buf() for _ in range(NS)]
        SFB = [Buf() for _ in range(4)]
        SBB = [Buf() for _ in range(4)]
        E2Ba, E2Bb = Buf(), Buf()
        EBs = [[SFB[0], SFB[1]], [SFB[2], SFB[3]], [E2Ba, E2Bb]]
        sf6 = sf + [E[2][:, 0:516], E[2][:, 516:1032]]
        SFB6 = SFB + [E2Ba, E2Bb]
        sbf3 = [ATT[:, 0:512], ATT[:, 512:1024], ATT[:, 1024:1536]]
        DGA, DGBB = Buf(), Buf()
        SPB = [[SBB[0], SBB[1]], [SBB[2], SBB[3]]]
        SSB = [Buf(), Buf()]
        WWB = [Buf(), Buf()]
        DGB = Buf()
        CONSTB = Buf()
        PARB = [Buf() for _ in range(2)]
        GQB = [Buf() for _ in range(2)]
        HALOB = Buf()
        PB = [Buf() for _ in range(8)]

        def xs(c, tt):
            return xT[:, c * S + tt * 512: c * S + tt * 512 + 512]

        def hs(c, t0, n, p0=0, pn=128):
            return hT[p0:p0 + pn, c * S + t0: c * S + t0 + n]

        def qn(c, t0, n, p0=0, pn=128):
            return R[p0:p0 + pn, c * S + t0: c * S + t0 + n]

        def kn(c, t0, n, p0=0, pn=128):
            return R[p0:p0 + pn, 8192 + c * S + t0: 8192 + c * S + t0 + n]

        def vv(tb, c0, n):
            return R[:, 16384 + tb * 512 + c0: 16384 + tb * 512 + c0 + n]

        def c0v(c, col, n):
            return R[:, 24576 + c * CST + col: 24576 + c * CST + col + n]

        def uv(j, t0, n):
            return R[:, j * 1024 + t0: j * 1024 + t0 + n]

        def pcol(l, i):
            return par[l % 2][:, i:i + 1]

        def MMG(bank, pairs, reads, first=True, last=True, out=None):
            n = len(pairs)
            if out is None:
                out = ps[bank][:]

            def fn(t):
                r = None
                for i, (l, rr) in enumerate(pairs):
                    r = t.matmul(out, l, rr, start=(first and i == 0), stop=(last and i == n - 1))
                return r
            return p.op("pe", fn, reads=list(reads), writes=[PB[bank]])

        def ACT(out, in_, func, reads, writes, bias=None, scale=None):
            kw = {}
            if bias is not None:
                kw["bias"] = bias
            if scale is not None:
                kw["scale"] = scale
            return p.op("act", lambda s: s.activation(out=out, in_=in_, func=func, **kw), reads=reads, writes=writes)

        def TT(eng, out, in0, in1, op, reads, writes):
            return p.op(eng, lambda v: v.tensor_tensor(out=out, in0=in0, in1=in1, op=op), reads=reads, writes=writes)

        def TS(eng, out, in0, s1, op0, reads, writes, s2=None, op1=None):
            if op1 is None:
                return p.op(eng, lambda v: v.tensor_scalar(out=out, in0=in0, scalar1=s1, scalar2=None, op0=op0), reads=reads, writes=writes)
            return p.op(eng, lambda v: v.tensor_scalar(out=out, in0=in0, scalar1=s1, scalar2=s2, op0=op0, op1=op1), reads=reads, writes=writes)

        def STT(eng, out, in0, scalar, in1, op0, op1, reads, writes):
            return p.op(eng, lambda v: v.scalar_tensor_tensor(out=out, in0=in0, scalar=scalar, in1=in1, op0=op0, op1=op1), reads=reads, writes=writes)

        def COPY(eng, out, in_, reads, writes):
            if eng == "act":
                return ACT(out, in_, AF.Copy, reads, writes)
            return p.op(eng, lambda v: v.tensor_copy(out=out, in_=in_), reads=reads, writes=writes)

        p.op("pool", lambda g: g.memset(ones[:], 1.0), writes=[CONSTB])
        p.op("pool", lambda g: g.memset(bdm[:], 0.0), writes=[CONSTB])
        p.op("pool", lambda g: g.memset(bdm[0:64, 0:64], 1.0), writes=[CONSTB])
        p.op("pool", lambda g: g.memset(bdm[64:128, 64:128], 1.0), writes=[CONSTB])
        p.op("pool", lambda g: g.memset(tri[:], 1.0), writes=[CONSTB])
        p.op("pool", lambda g: g.affine_select(out=tri[:], in_=tri[:], pattern=[[-1, 128]], compare_op=ALU.is_ge,
                                               fill=0.0, base=0, channel_multiplier=1), reads=[CONSTB], writes=[CONSTB])
        p.op("pool", lambda g: g.memset(ident[:], 1.0), writes=[CONSTB])
        p.op("pool", lambda g: g.affine_select(out=ident[:], in_=ident[:], pattern=[[-1, 128]], compare_op=ALU.is_equal,
                                               fill=0.0, base=0, channel_multiplier=1), reads=[CONSTB], writes=[CONSTB])
        p.op("pool", lambda g: g.memset(maskw[:], 0.0), writes=[CONSTB])
        p.op("pool", lambda g: g.affine_select(out=maskw[:], in_=maskw[:], pattern=[[1, 896]], compare_op=ALU.is_gt,
                                               fill=-1024.0, base=-384, channel_multiplier=-1), reads=[CONSTB], writes=[CONSTB])
        for c in range(4):
            p.op("pool", lambda g, c=c: g.memset(c0v(c, 0, 32), 0.0), writes=[CONSTB])
        p.barrier()
        CONSTB.const = True

        groups = []
        for l in range(n_layers):
            for g in range(5):
                groups.append((win_d[l * 5 + g], 4096))
            for g in range(2):
                groups.append((wout_d[l * 2 + g], 4096))
            for hf in range(2):
                for g in range(11):
                    groups.append((wup_d[l * 11 + g], 4096))
                for g in range(8):
                    groups.append((wdn_d[l * 8 + g], 2816))
        wst = {"issued": 0, "done": 0, "cur": 0}

        def w_issue():
            while wst["issued"] < len(groups) and wst["issued"] - NS < wst["done"]:
                n = wst["issued"]
                src, ncol = groups[n]
                slot = n % NS
                p.op("pool", lambda g, src=src, ncol=ncol, slot=slot: g.dma_start(out=wsl[slot][:, 0:ncol], in_=src),
                     writes=[WB[slot]], dma_sem="w%d" % slot)
                wst["issued"] += 1

        def w_get():
            n = wst["cur"]
            assert n < wst["issued"], "weight group not issued"
            wst["cur"] += 1
            return n % NS

        def w_done():
            wst["done"] += 1
            w_issue()

        def tt_view(ap2, tt):
            return bass.AP(ap2.tensor, ap2.offset + tt * 512, [ap2.ap[0], [S, 8], [1, 512]])
        xT_all = xT[:]
        p.op("sp", lambda g: g.dma_start(out=par[0][:], in_=par_d[0]), writes=[PARB[0]], dma_sem="par0")
        for tt in range(4):
            p.op("sp", lambda g, tt=tt: g.dma_start(out=tt_view(xT_all, tt), in_=tt_view(xin, tt)),
                 writes=[XB[c][tt] for c in range(8)], dma_sem="x%d" % tt)
        w_issue()
        outs = []

        bank_all = Rot(range(8))

        def rmsnorm(l, tts, gcol0):
            for tt in tts:
                bank = bank_all.next()
                for c in range(8):
                    i = c % 2
                    ACT(sbf[i][:], xs(c, tt), AF.Square, [XB[c][tt]], [SBB[i]])
                    MMG(bank, [(ones[:], sbf[i][:])], [SBB[i], CONSTB], first=(c == 0), last=(c == 7))
                a = tt % 2
                ACT(sf[a][:, 0:512], ps[bank][:], AF.Ln, [PB[bank]], [SFB[a]], bias=EPS, scale=1.0 / D)
                ACT(sf[a][:, 0:512], sf[a][:, 0:512], AF.Exp, [SFB[a]], [SFB[a]], scale=-0.5)
                for c in range(8):
                    STT("dve", hs(c, tt * 512, 512), xs(c, tt), pcol(l, gcol0 + c), sf[a][:, 0:512], ALU.mult, ALU.mult,
                        [XB[c][tt], SFB[a], PARB[l % 2]], [HB[c][tt]])

        for l in range(n_layers):
            pl = l % 2
            if l + 1 < n_layers:
                p.op("sp", lambda g, l=l: g.dma_start(out=par[(l + 1) % 2][:], in_=par_d[l + 1]), writes=[PARB[(l + 1) % 2]],
                     dma_sem="par%d" % ((l + 1) % 2))
            TS("pool", gqs[pl][:], pcol(l, 8), 0.125, ALU.mult, [PARB[pl]], [GQB[pl]])

            if stop_after == 'A0':
                break
            rmsnorm(l, range(4) if l == 0 else [2, 3], 0)
            p.barrier()

            if stop_after == 'A1':
                break
            for g in range(2):
                slot = w_get()
                W = wsl[slot]
                items = [(oc, tt) for oc in range(4) for tt in range(4)]
                banks_qk = {}

                def qk_main(idx, W=W, slot=slot):
                    oc, tt = items[idx]
                    bank = bank_all.next()
                    banks_qk[idx] = bank
                    MMG(bank, [(W[:, c * 512 + oc * 128: c * 512 + oc * 128 + 128], hs(c, tt * 512, 512)) for c in range(8)],
                        [WB[slot]] + [HB[c][tt] for c in range(8)])
                    i = idx % 3
                    COPY("dve", sf6[2 * i][:, 0:512], ps[bank][:], [PB[bank]], [SFB6[2 * i]])
                    ACT(sbf3[i], sf6[2 * i][:, 0:512], AF.Square, [SFB6[2 * i]], [SBB[i]])

                def qk_tail(idx, g=g):
                    oc, tt = items[idx]
                    i = idx % 3
                    bank2 = bank_all.next()
                    MMG(bank2, [(bdm[:], sbf3[i])], [SBB[i], CONSTB])
                    ACT(sf6[2 * i + 1][:, 0:512], ps[bank2][:], AF.Ln, [PB[bank2]], [SFB6[2 * i + 1]], bias=EPS, scale=1.0 / 64)
                    ACT(sf6[2 * i + 1][:, 0:512], sf6[2 * i + 1][:, 0:512], AF.Exp, [SFB6[2 * i + 1]], [SFB6[2 * i + 1]], scale=-0.5)

                def qk_fin(idx, g=g):
                    oc, tt = items[idx]
                    i = idx % 3
                    if g == 0:
                        STT("dve", qn(oc, tt * 512, 512), sf6[2 * i][:, 0:512], gqs[pl][:], sf6[2 * i + 1][:, 0:512], ALU.mult, ALU.mult,
                            [SFB6[2 * i], SFB6[2 * i + 1], GQB[pl]], [QB[oc][tt]])
                    else:
                        STT("dve", kn(oc, tt * 512, 512), sf6[2 * i][:, 0:512], pcol(l, 9), sf6[2 * i + 1][:, 0:512], ALU.mult, ALU.mult,
                            [SFB6[2 * i], SFB6[2 * i + 1], PARB[pl]], [KB[oc][tt]])

                NI = len(items)
                for idx in range(NI + 3):
                    if 0 <= idx - 3 < NI:
                        qk_fin(idx - 3)
                    if 0 <= idx - 2 < NI:
                        qk_tail(idx - 2)
                    if idx < NI:
                        qk_main(idx)
                w_done()
            slot = w_get()
            W = wsl[slot]
            for tb in range(16):
                bank = bank_all.next()
                MMG(bank, [(hs(c, tb * 128, 128), W[:, c * 512: c * 512 + 512]) for c in range(8)],
                    [WB[slot]] + [HB[c][tb // 4] for c in range(8)])
                COPY("act" if tb % 2 == 0 else "dve", vv(tb, 0, 512), ps[bank][:], [PB[bank]], [VB[tb]])
            w_done()
            for gg in range(2):
                slot = w_get()
                W = wsl[slot]
                for o2 in range(2):
                    oc = 2 * gg + o2
                    for tt in range(4):
                        ba = bank_all.next()
                        bb = bank_all.next()
                        rd = [WB[slot]] + [HB[c][tt] for c in range(8)]
                        MMG(ba, [(W[:, c * 512 + o2 * 128: c * 512 + o2 * 128 + 128], hs(c, tt * 512, 512)) for c in range(8)], rd)
                        MMG(bb, [(W[:, c * 512 + 256 + o2 * 128: c * 512 + 256 + o2 * 128 + 128], hs(c, tt * 512, 512)) for c in range(8)], rd)
                        i = (oc * 4 + tt) % 2
                        ACT(sf[i][:, 0:512], ps[bb][:], AF.Sigmoid, [PB[bb]], [SFB[i]])
                        TT("dve", c0v(oc, 32 + tt * 512, 512), ps[ba][:], sf[i][:, 0:512], ALU.mult, [PB[ba], SFB[i]], [C0B[oc][tt]])
                w_done()

            if stop_after == 'A2':
                break
            idv = ident[:]

            def dbuild(c, k0, k1, buf):
                wv = par[pl][:, 10 + c * 31 + k0: 10 + c * 31 + k1]
                dgv = diag[:, k0 * 128:k1 * 128]
                d_in0 = bass.AP(idv.tensor, idv.offset, [idv.ap[0], [0, k1 - k0], idv.ap[1]])
                d_in1 = bass.AP(wv.tensor, wv.offset, [wv.ap[0], wv.ap[1], [0, 128]])
                d_out = bass.AP(dgv.tensor, dgv.offset, [dgv.ap[0], [128, k1 - k0], [1, 128]])
                p.op("dve", lambda v, a=d_out, b=d_in0, cc=d_in1: v.tensor_tensor(out=a, in0=b, in1=cc, op=ALU.mult),
                     reads=[CONSTB, PARB[pl]], writes=[buf])

            def conv_chunk(half, c):
                tts = [2 * half, 2 * half + 1]
                banks = {tt: bank_all.next() for tt in tts}
                dbuild(c, 0, 16, DGA)
                for tt in tts:
                    rd = [DGA, C0B[c][tt]] + ([C0B[c][tt - 1]] if tt else [])
                    MMG(banks[tt], [(diag[:, k * 128:(k + 1) * 128], c0v(c, tt * 512 + k + 2, 512)) for k in range(0, 16)], rd, first=True, last=False)
                dbuild(c, 16, 31, DGBB)
                for tt in tts:
                    rd = [DGBB, C0B[c][tt]] + ([C0B[c][tt - 1]] if tt else [])
                    MMG(banks[tt], [(diag[:, k * 128:(k + 1) * 128], c0v(c, tt * 512 + k + 2, 512)) for k in range(16, 31)], rd, first=False, last=True)
                for tt in tts:
                    ACT(hs(4 + c, tt * 512, 512), ps[banks[tt]][:], AF.Identity, [PB[banks[tt]], PARB[pl]], [HB[4 + c][tt]], bias=pcol(l, 134 + c))

            def ln_steps(half):
                steps = []
                bms, bss, MR = {}, {}, {}

                def yv(c, tt):
                    return hs(4 + c, tt * 512, 512)

                def st_a(t2):
                    tt = 2 * half + t2
                    bms[t2] = bank_all.next()
                    bss[t2] = bank_all.next()
                    for c in range(4):
                        i = c % 2
                        ACT(sbf[i][:], yv(c, tt), AF.Square, [HB[4 + c][tt]], [SBB[i]])
                        MMG(bms[t2], [(ones[:], yv(c, tt))], [HB[4 + c][tt], CONSTB], first=(c == 0), last=(c == 3))
                        MMG(bss[t2], [(ones[:], sbf[i][:])], [SBB[i], CONSTB], first=(c == 0), last=(c == 3))
                steps.append(lambda: st_a(0))
                steps.append(lambda: st_a(1))

                def st_b():
                    for t2 in range(2):
                        mi, ri = (0, 1) if t2 == 0 else (4, 5)
                        MR[t2] = (sf6[mi][:, 0:512], sf6[ri][:, 0:512], mi, ri)
                        ACT(MR[t2][0], ps[bms[t2]][:], AF.Copy, [PB[bms[t2]]], [SFB6[mi]], scale=1.0 / 512)
                    for t2 in range(2):
                        M_, R_, mi, ri = MR[t2]
                        TT("dve", R_, M_, M_, ALU.mult, [SFB6[mi]], [SFB6[ri]])
                        STT("dve", R_, ps[bss[t2]][:], 1.0 / 512, R_, ALU.mult, ALU.subtract, [PB[bss[t2]], SFB6[ri]], [SFB6[ri]])
                    for t2 in range(2):
                        M_, R_, mi, ri = MR[t2]
                        ACT(R_, R_, AF.Ln, [SFB6[ri]], [SFB6[ri]], bias=EPS)
                    for t2 in range(2):
                        M_, R_, mi, ri = MR[t2]
                        ACT(R_, R_, AF.Exp, [SFB6[ri]], [SFB6[ri]], scale=-0.5)
                steps.append(st_b)

                def st_c(c, t2, n_it):
                    tt = 2 * half + t2
                    M_, R_, mi, ri = MR[t2]
                    i = 2 + n_it % 2
                    TT("dve", sf[i][:, 0:512], yv(c, tt), M_, ALU.subtract, [HB[4 + c][tt], SFB6[mi]], [SFB[i]])
                    TT("dve" if n_it % 2 == 0 else "pool", sf[i][:, 0:512], sf[i][:, 0:512], R_, ALU.mult, [SFB[i], SFB6[ri]], [SFB[i]])
                    ACT(yv(c, tt), sf[i][:, 0:512], AF.Silu, [SFB[i], PARB[pl]], [HB[4 + c][tt]],
                        bias=pcol(l, 142 + c), scale=pcol(l, 138 + c))
                n_it = 0
                for c in range(4):
                    for t2 in range(2):
                        steps.append(lambda c=c, t2=t2, n_it=n_it: st_c(c, t2, n_it))
                        n_it += 1
                return steps

            for c in range(4):
                conv_chunk(0, c)
            pend = ln_steps(0)
            per = [3, 3, 3, 2]
            for c in range(4):
                conv_chunk(1, c)
                for _ in range(per[c]):
                    pend.pop(0)()
            assert not pend
            for st in ln_steps(1):
                st()
            if stop_after == 'A3':
                break
            p.barrier()
            pairs = []
            seqi = 0
            for c in range(4):
                for qt in range(4):
                    kbs = list(range(4 * qt + 3, -1, -1))
                    for n, kb in enumerate(kbs):
                        pairs.append(dict(c=c, qt=qt, kb=kb, first=(n == 0), last=(n == len(kbs) - 1), ob=6 + seqi % 2))
                    seqi += 1
            NU = len(pairs)

            def pv(ap2, a):
                return bass.AP(ap2.tensor, ap2.offset + a, [ap2.ap[0], [512, 2], [1, 512 - a]])

            def acol(i):
                u = pairs[i]
                j = u["kb"] - 4 * u["qt"]
                return 128 * j if j > 0 else 0

            def P1(i):
                u = pairs[i]
                c, qt, kb = u["c"], u["qt"], u["kb"]
                zb = 0
                j = kb - 4 * qt
                dg = j >= 0
                a = acol(i)
                n = 512 - a
                for hh in range(2):
                    MMG(zb + hh, [(kn(c, kb * 128, 128, hh * 64, 64), qn(c, qt * 512 + a, n, hh * 64, 64))], [KB[c][kb // 4], QB[c][qt]],
                        last=not dg, out=ps[zb + hh][:, a:512])
                if dg:
                    mw = maskw[:, 384 - 128 * j + a: 384 - 128 * j + 512]
                    for hh in range(2):
                        MMG(zb + hh, [(ident[:], mw)], [CONSTB], first=False, out=ps[zb + hh][:, a:512])
                ACT(pv(E[i % 3][:, 0:1024], a), pv(PSALL[:, zb * 512: zb * 512 + 1024], a), AF.Exp, [PB[zb], PB[zb + 1]], EBs[i % 3])

            def P1b(i):
                u = pairs[i]
                a = acol(i)
                ACT(pv(SP[i % 2], a), pv(E[i % 3][:, 0:1024], a), AF.Ln, EBs[i % 3], SPB[i % 2], bias=1.0)
                if not u["last"]:
                    if u["first"]:
                        COPY("pool", pv(SS[i % 2], a), pv(SP[i % 2], a), SPB[i % 2], [SSB[i % 2]])
                    else:
                        ap_ = acol(i - 1)
                        TT("pool", pv(SS[i % 2], ap_), pv(SS[(i - 1) % 2], ap_), pv(SP[i % 2], ap_), ALU.add, [SSB[(i - 1) % 2]] + SPB[i % 2], [SSB[i % 2]])
                        if ap_ > a:
                            def v2(ap2, lo, hi):
                                return bass.AP(ap2.tensor, ap2.offset + lo, [ap2.ap[0], [512, 2], [1, hi - lo]])
                            COPY("pool", v2(SS[i % 2], a, ap_), v2(SP[i % 2], a, ap_), SPB[i % 2], [SSB[i % 2]])

            def P2(i):
                u = pairs[i]
                cb = 2 + 2 * (i % 2)
                a = acol(i)
                for hh in range(2):
                    spv = SP[i % 2][:, hh * 512 + a:(hh + 1) * 512]
                    if u["first"]:
                        MMG(cb + hh, [(tri[:], spv)], SPB[i % 2] + [CONSTB], out=ps[cb + hh][:, a:512])
                    else:
                        ap_ = acol(i - 1)
                        MMG(cb + hh, [(tri[:], spv)], SPB[i % 2] + [CONSTB], last=False, out=ps[cb + hh][:, a:512])
                        MMG(cb + hh, [(ones[:], SS[(i - 1) % 2][:, hh * 512 + ap_:(hh + 1) * 512])], [SSB[(i - 1) % 2], CONSTB],
                            first=False, out=ps[cb + hh][:, ap_:512])

            def P2r(i):
                cb = 2 + 2 * (i % 2)
                a = acol(i)
                ACT(pv(WW[i % 2], a), pv(PSALL[:, cb * 512: cb * 512 + 1024], a), AF.Exp, [PB[cb], PB[cb + 1]], [WWB[i % 2]], scale=-1.0)

            def P2b(i):
                a = acol(i)
                TT("dve", pv(WW[i % 2], a), pv(E[i % 3][:, 0:1024], a), pv(WW[i % 2], a), ALU.mult, EBs[i % 3] + [WWB[i % 2]], [WWB[i % 2]])

            def P3(i):
                u = pairs[i]
                c, qt, kb = u["c"], u["qt"], u["kb"]
                ob = u["ob"]
                a = acol(i)

                def fn(t, ob=ob, kb=kb, c=c, i=i, a=a, first=u["first"], last=u["last"]):
                    t.matmul(ps[ob][0:64, a:512], vv(kb, c * 128, 64), WW[i % 2][:, a:512], start=first, stop=last, skip_group_check=True)
                    return t.matmul(ps[ob][64:128, a:512], vv(kb, c * 128 + 64, 64), WW[i % 2][:, 512 + a:1024], start=first, stop=last, skip_group_check=True)
                p.op("pe", fn, reads=[VB[kb], WWB[i % 2]], writes=[PB[ob]])
                if u["last"]:
                    COPY("dve", hs(c, qt * 512, 512), ps[ob], [PB[ob]], [HB[c][qt]])

            for it in range(NU + 3):
                if 0 <= it - 2 < NU:
                    P2r(it - 2)
                if it < NU:
                    P1(it)
                if 0 <= it - 1 < NU:
                    P2(it - 1)
                if it < NU:
                    P1b(it)
                if 0 <= it - 2 < NU:
                    P2b(it - 2)
                if 0 <= it - 3 < NU:
                    P3(it - 3)
            p.barrier()

            if stop_after == 'A4':
                break
            for g in range(2):
                slot = w_get()
                W = wsl[slot]
                for o4 in range(4):
                    oc = g * 4 + o4
                    for tt in range(4):
                        bank = bank_all.next()
                        MMG(bank, [(W[:, c * 512 + o4 * 128: c * 512 + o4 * 128 + 128], hs(c, tt * 512, 512)) for c in range(8)],
                            [WB[slot]] + [HB[c][tt] for c in range(8)])
                        TT("dve", xs(oc, tt), xs(oc, tt), ps[bank][:], ALU.add, [XB[oc][tt], PB[bank]], [XB[oc][tt]])
                w_done()

            if stop_after == 'A5':
                break
            p.op("pool", lambda g: g.memset(halo[:], 0.0), writes=[HALOB])
            rmsnorm(l, range(4), 146)
            for hf in range(2):
                pend_u = []
                pend_s = []
                for pg in range(11):
                    slot = w_get()
                    W = wsl[slot]
                    for jj in range(2):
                        j = 2 * pg + jj
                        for t2 in range(2):
                            tt = 2 * hf + t2
                            bg = bank_all.next()
                            bv = bank_all.next()
                            rd = [WB[slot]] + [HB[c][tt] for c in range(8)]
                            MMG(bg, [(W[:, c * 512 + jj * 128: c * 512 + jj * 128 + 128], hs(c, tt * 512, 512)) for c in range(8)], rd)
                            MMG(bv, [(W[:, c * 512 + 256 + jj * 128: c * 512 + 256 + jj * 128 + 128], hs(c, tt * 512, 512)) for c in range(8)], rd)
                            i = (j * 2 + t2) % 2
                            gb, t1 = sf[i], sf[2 + i]
                            COPY("pool", gb[:, 0:2], halo[:, 2 * j:2 * j + 2], [HALOB], [SFB[i]])
                            ACT(gb[:, 2:514], ps[bg][:], AF.Copy, [PB[bg]], [SFB[i]])
                            COPY("pool", halo[:, 2 * j:2 * j + 2], gb[:, 512:514], [SFB[i]], [HALOB])
                            if len(pend_u) > 1:
                                pend_u.pop(0)()
                            wc = 154 + j * 3
                            TS("dve", t1[:, 0:512], gb[:, 2:514], pcol(l, wc + 2), ALU.mult, [SFB[i], PARB[pl]], [SFB[2 + i]])
                            STT("dve", t1[:, 0:512], gb[:, 1:513], pcol(l, wc + 1), t1[:, 0:512], ALU.mult, ALU.add, [SFB[i], SFB[2 + i], PARB[pl]], [SFB[2 + i]])
                            STT("dve", t1[:, 0:512], gb[:, 0:512], pcol(l, wc + 0), t1[:, 0:512], ALU.mult, ALU.add, [SFB[i], SFB[2 + i], PARB[pl]], [SFB[2 + i]])
                            if pend_s:
                                pend_s.pop(0)()
                            pend_s.append(lambda j=j, t1=t1, i=i: ACT(t1[:, 0:512], t1[:, 0:512], AF.Silu, [SFB[2 + i], PARB[pl]], [SFB[2 + i]], bias=pcol(l, 220 + j)))
                            pend_u.append(lambda j=j, t2=t2, t1=t1, bv=bv, i=i: TT("dve", uv(j, t2 * 512, 512), t1[:, 0:512], ps[bv][:], ALU.mult,
                                                                                  [SFB[2 + i], PB[bv]], [UB[j][t2]]))
                    w_done()
                while pend_s:
                    pend_s.pop(0)()
                while pend_u:
                    pend_u.pop(0)()
                for oc in range(8):
                    slot = w_get()
                    W = wsl[slot]
                    for t2 in range(2):
                        tt = 2 * hf + t2
                        bank = bank_all.next()
                        MMG(bank, [(W[:, j * 128:(j + 1) * 128], uv(j, t2 * 512, 512)) for j in range(NJ)],
                            [WB[slot]] + [UB[j][t2] for j in range(NJ)])
                        TT("dve", xs(oc, tt), xs(oc, tt), ps[bank][:], ALU.add, [XB[oc][tt], PB[bank]], [XB[oc][tt]])
                    w_done()
                if l + 1 < n_layers:
                    if hf == 0:
                        rmsnorm(l + 1, [0, 1], 0)
                elif not dbg:
                    for t2 in range(2):
                        tt = 2 * hf + t2
                        outs.append(p.op("sp", lambda g, tt=tt: g.dma_start(out=tt_view(yout, tt), in_=tt_view(xT_all, tt)),
                                         reads=[XB[c][tt] for c in range(8)], dma_sem="out"))

        if dbg:
            p.barrier()
            p.op("sp", lambda g: g.dma_start(out=dbg_h, in_=hT[:]), dma_sem="out")
            p.op("sp", lambda g: g.dma_start(out=dbg_r, in_=R[:]), dma_sem="out")
        if not outs:
            for tt in range(4):
                outs.append(p.op("sp", lambda g, tt=tt: g.dma_start(out=tt_view(yout, tt), in_=tt_view(xT_all, tt)),
                                 reads=[XB[c][tt] for c in range(8)], dma_sem="out"))
        p.emit(final_waits=[outs[-1]])
        build_program.last_stats = {q: (len(s), sum(len(o.waits) for o in s)) for q, s in p.streams.items()}
    return nc


def _pm(w, kc):
    n = w.shape[1]
    return np.ascontiguousarray(w.reshape(kc, 128, n).transpose(1, 0, 2).reshape(128, kc * n))


def prep_weights(inp, n_layers=DEPTH):
    f = lambda a: np.asarray(a, dtype=np.float32)
    w_in, w_out, w_up, w_down = f(inp["w_in"]), f(inp["w_out"]), f(inp["w_up"]), f(inp["w_down"])
    win = []
    for l in range(n_layers):
        for g in range(3):
            win.append(_pm(w_in[l][:, g * 512:(g + 1) * 512], 8))
        for gg in range(2):
            cols = np.concatenate([w_in[l][:, 1536 + gg * 256: 1536 + (gg + 1) * 256], w_in[l][:, 2048 + gg * 256: 2048 + (gg + 1) * 256]], axis=1)
            win.append(_pm(cols, 8))
    win = np.stack(win)
    wout = np.stack([_pm(w_out[l][:, g * 512:(g + 1) * 512], 8) for l in range(n_layers) for g in range(2)])
    wup = []
    for l in range(n_layers):
        for pg in range(11):
            cols = np.concatenate([w_up[l][:, pg * 256:(pg + 1) * 256], w_up[l][:, DFF + pg * 256: DFF + (pg + 1) * 256]], axis=1)
            wup.append(_pm(cols, 8))
    wup = np.stack(wup)
    wdn = np.stack([_pm(w_down[l][:, oc * 128:(oc + 1) * 128], NJ) for l in range(n_layers) for oc in range(8)])
    par = np.zeros((n_layers, 128, NPAR), np.float32)
    for l in range(n_layers):
        P = par[l]
        P[:, 0:8] = f(inp["norm1_g"])[l].reshape(8, 128).T
        P[:, 8] = np.tile(f(inp["q_norm_g"])[l], 2)
        P[:, 9] = np.tile(f(inp["k_norm_g"])[l], 2)
        cw = f(inp["conv_dw_w"])[l]
        for c in range(4):
            P[:, 10 + c * 31: 10 + (c + 1) * 31] = cw[:, c * 128:(c + 1) * 128].T
        P[:, 134:138] = f(inp["conv_dw_b"])[l].reshape(4, 128).T
        P[:, 138:142] = f(inp["conv_ln_g"])[l].reshape(4, 128).T
        P[:, 142:146] = f(inp["conv_ln_b"])[l].reshape(4, 128).T
        P[:, 146:154] = f(inp["norm2_g"])[l].reshape(8, 128).T
        fw = f(inp["ffn_dw_w"])[l]
        for j in range(NJ):
            P[:, 154 + j * 3: 154 + j * 3 + 3] = fw[:, j * 128:(j + 1) * 128].T
        P[:, 220:242] = f(inp["ffn_dw_b"])[l].reshape(NJ, 128).T
    return dict(par=par, win=win, wout=wout, wup=wup, wdn=wdn)


def prep_x(xb):
    return np.ascontiguousarray(np.asarray(xb, np.float32).T.reshape(8, 128, S).transpose(1, 0, 2).reshape(128, 8 * S))


def unprep_y(y):
    return np.ascontiguousarray(y.reshape(128, 8, S).transpose(1, 0, 2).reshape(D, S).T)


_NC_CACHE = {}


def kernel(**inputs):
    x = np.asarray(inputs["x"], dtype=np.float32)
    B = x.shape[0]
    if DEPTH not in _NC_CACHE:
        _NC_CACHE[DEPTH] = build_program(DEPTH)
    nc = _NC_CACHE[DEPTH]
    wts = prep_weights(inputs, DEPTH)
    in_maps = []
    for b in range(B):
        m = dict(wts)
        m["xin"] = prep_x(x[b])
        in_maps.append(m)
    res = run_bass_kernel_spmd(nc, in_maps, core_ids=list(range(B)))
    out = np.stack([unprep_y(np.asarray(res.results[b]["yout"], dtype=np.float32)) for b in range(B)])
    return out.astype(np.float32)
```

```python
import contextlib
import numpy as np
import concourse.bass as bass
import concourse.mybir as mybir
from concourse.bass_utils import run_bass_kernel_spmd

F32 = mybir.dt.float32
BF16 = mybir.dt.bfloat16
AF = mybir.ActivationFunctionType
ALU = mybir.AluOpType

S = 2048
D = 1024
DEPTH = 4
NH = 8
DFF = 2816
NJ = DFF // 128
EPS = 1e-6
NPAR = 242
N_CORES = 8

COMPUTE = ("pe", "act", "dve", "pool")
QUEUES = ("pe", "act", "dve", "pool", "sp")


class Buf:
    __slots__ = ("writer", "readers", "const")

    def __init__(self):
        self.writer = None
        self.readers = []
        self.const = False


class Op:
    __slots__ = ("eng", "fn", "pos", "waits", "sig", "vc", "is_dma", "sem", "val", "ndma", "sigval")

    def __init__(self, eng, fn):
        self.eng = eng
        self.fn = fn
        self.waits = []
        self.sig = False
        self.is_dma = False
        self.sem = None
        self.val = 0
        self.ndma = 0
        self.sigval = None


class Prog:
    def __init__(self, nc, same_engine_dist=2):
        self.nc = nc
        self.streams = {q: [] for q in QUEUES}
        self.clk = {q: {} for q in QUEUES}
        self.selfw = {q: -1 for q in QUEUES}
        self.dma_sems = {}
        self.eng_sems = {}
        self.same_engine_dist = same_engine_dist
        self.fence = {}

    def _need(self, o, d):
        if d.is_dma:
            return d.val > self.clk[o.eng].get(d.sem, 0)
        if d.eng == o.eng:
            if o.eng == "pe":
                return False
            if d.pos <= self.selfw[o.eng]:
                return False
            if o.is_dma:
                return True
            return (o.pos - d.pos) <= self.same_engine_dist
        return d.pos > self.clk[o.eng].get(d.eng, -1)

    def barrier(self):
        last = [s[-1] for q, s in self.streams.items() if s and q in COMPUTE]
        for q in QUEUES:
            self.fence[q] = list(last)

    def op(self, eng, fn, reads=(), writes=(), dma_sem=None, ndma=1):
        o = Op(eng, fn)
        o.pos = len(self.streams[eng])
        if dma_sem is not None:
            o.is_dma = True
            o.sem = dma_sem
            o.ndma = ndma
            st = self.dma_sems.setdefault(dma_sem, [None, 0])
            st[1] += 16 * ndma
            o.val = st[1]
        deps = []
        for b in reads:
            if b.writer is not None:
                deps.append(b.writer)
        for b in writes:
            if b.writer is not None:
                deps.append(b.writer)
            deps.extend(b.readers)
        f = self.fence.get(eng)
        if f:
            deps.extend(f)
            self.fence[eng] = None
        clk = self.clk[eng]
        seen = set()
        for d in deps:
            if d is o or id(d) in seen:
                continue
            seen.add(id(d))
            if self._need(o, d):
                o.waits.append(d)
                d.sig = True
                for k, v in d.vc.items():
                    if v > clk.get(k, -1):
                        clk[k] = v
                if d.eng == eng and not d.is_dma:
                    self.selfw[eng] = max(self.selfw[eng], d.pos)
        vc = dict(clk)
        if o.is_dma:
            vc[o.sem] = o.val
        else:
            vc[eng] = o.pos
        o.vc = vc
        for b in reads:
            if not b.const:
                b.readers.append(o)
        for b in writes:
            b.writer = o
            b.readers = []
        self.streams[eng].append(o)
        return o

    def emit(self, final_waits=()):
        nc = self.nc
        with contextlib.ExitStack() as ctx:
            for q in COMPUTE:
                self.eng_sems[q] = ctx.enter_context(nc.semaphore("s_" + q))
            for name, st in self.dma_sems.items():
                st[0] = ctx.enter_context(nc.semaphore("d_" + name))
            for d in final_waits:
                d.sig = True
            for q in COMPUTE:
                c = 0
                for o in self.streams[q]:
                    if o.sig and not o.is_dma:
                        c += 1
                        o.sigval = c
            block = ctx.enter_context(nc.Block())

            def run(q):
                def body(eng):
                    def wait(d):
                        if d.is_dma:
                            eng.wait_ge(self.dma_sems[d.sem][0], d.val)
                        else:
                            eng.wait_ge(self.eng_sems[d.eng], d.sigval)
                    for o in self.streams[q]:
                        for d in o.waits:
                            wait(d)
                        r = o.fn(eng)
                        if o.is_dma:
                            insts = r if isinstance(r, (list, tuple)) else [r]
                            assert len(insts) == o.ndma
                            for i in insts:
                                i.then_inc(self.dma_sems[o.sem][0], 16)
                        elif o.sig:
                            r.then_inc(self.eng_sems[q], 1)
                    if q == "sp":
                        for d in final_waits:
                            wait(d)
                return body

            block.tensor(run("pe"))
            block.scalar(run("act"))
            block.vector(run("dve"))
            block.gpsimd(run("pool"))
            block.sync(run("sp"))


class Rot:
    def __init__(self, items):
        self.items = list(items)
        self.i = 0

    def next(self):
        r = self.items[self.i % len(self.items)]
        self.i += 1
        return r


def build_program(n_layers=DEPTH, attn_debug=None, stop_after=None, dbg=None):
    nc = bass.Bass("TRN2", target_bir_lowering=False)
    xin = nc.dram_tensor("xin", [128, 8 * S], F32, kind="ExternalInput").ap()
    par_d = nc.dram_tensor("par", [n_layers, 128, NPAR], F32, kind="ExternalInput").ap()
    win_d = nc.dram_tensor("win", [n_layers * 5, 128, 4096], F32, kind="ExternalInput").ap()
    wout_d = nc.dram_tensor("wout", [n_layers * 2, 128, 4096], F32, kind="ExternalInput").ap()
    wup_d = nc.dram_tensor("wup", [n_layers * 11, 128, 4096], F32, kind="ExternalInput").ap()
    wdn_d = nc.dram_tensor("wdn", [n_layers * 8, 128, 2816], F32, kind="ExternalInput").ap()
    yout = nc.dram_tensor("yout", [128, 8 * S], F32, kind="ExternalOutput").ap()
    if dbg:
        dbg_h = nc.dram_tensor("dbg_h", [128, 8 * S], BF16, kind="ExternalOutput").ap()
        dbg_r = nc.dram_tensor("dbg_r", [128, 3 * 8192 + 4 * 2080], BF16, kind="ExternalOutput").ap()

    with contextlib.ExitStack() as ctx:
        def sb(name, shape, dt):
            return ctx.enter_context(nc.sbuf_tensor(name, shape, dt))

        xT = sb("xT", [128, 8 * S], F32)
        hT = sb("hT", [128, 8 * S], BF16)
        CST = 2080
        R = sb("R", [128, 3 * 8192 + 4 * CST], BF16)
        NS = 2
        wsl = [sb("wsl%d" % i, [128, 4096], BF16) for i in range(NS)]
        E = [sb("E%d" % i, [128, 1032], F32) for i in range(3)]
        ATT = sb("ATT", [128, 6144], BF16)
        sf = [E[0][:, 0:516], E[0][:, 516:1032], E[1][:, 0:516], E[1][:, 516:1032]]
        sbf = [ATT[:, 0:512], ATT[:, 512:1024]]
        diag = ATT[:, 2048:2048 + 31 * 128]
        SP = [ATT[:, 0:1024], ATT[:, 1024:2048]]
        SS = [ATT[:, 2048:3072], ATT[:, 3072:4096]]
        WW = [ATT[:, 4096:5120], ATT[:, 5120:6144]]
        ones = sb("ones", [128, 128], BF16)
        bdm = sb("bdm", [128, 128], BF16)
        tri = sb("tri", [128, 128], BF16)
        ident = sb("ident", [128, 128], BF16)
        maskw = sb("maskw", [128, 896], BF16)
        par = [sb("par%d" % i, [128, NPAR], F32) for i in range(2)]
        gqs = [sb("gqs%d" % i, [128, 1], F32) for i in range(2)]
        halo = sb("halo", [128, NJ * 2], F32)
        PSALL = ctx.enter_context(nc.psum_tensor("psall", [128, 4096], F32))
        ps = [PSALL[:, b * 512:(b + 1) * 512] for b in range(8)]

        p = Prog(nc)
        XB = [[Buf() for _ in range(4)] for _ in range(8)]
        HB = [[Buf() for _ in range(4)] for _ in range(8)]
        QB = [[Buf() for _ in range(4)] for _ in range(4)]
        KB = [[Buf() for _ in range(4)] for _ in range(4)]
        VB = [Buf() for _ in range(16)]
        C0B = [[Buf() for _ in range(4)] for _ in range(4)]
        UB = [[Buf() for _ in range(2)] for _ in range(NJ)]
        WB = [Buf() for _ in range(NS)]
        SFB = [Buf() for _ in range(4)]
        SBB = [Buf() for _ in range(4)]
        E2Ba, E2Bb = Buf(), Buf()
        EBs = [[SFB[0], SFB[1]], [SFB[2], SFB[3]], [E2Ba, E2Bb]]
        sf6 = sf + [E[2][:, 0:516], E[2][:, 516:1032]]
        SFB6 = SFB + [E2Ba, E2Bb]
        sbf3 = [ATT[:, 0:512], ATT[:, 512:1024], ATT[:, 1024:1536]]
        DGA, DGBB = Buf(), Buf()
        SPB = [[SBB[0], SBB[1]], [SBB[2], SBB[3]]]
        SSB = [Buf(), Buf()]
        WWB = [Buf(), Buf()]
        DGB = Buf()
        CONSTB = Buf()
        PARB = [Buf() for _ in range(2)]
        GQB = [Buf() for _ in range(2)]
        HALOB = Buf()
        PB = [Buf() for _ in range(8)]

        def xs(c, tt):
            return xT[:, c * S + tt * 512: c * S + tt * 512 + 512]

        def hs(c, t0, n, p0=0, pn=128):
            return hT[p0:p0 + pn, c * S + t0: c * S + t0 + n]

        def qn(c, t0, n, p0=0, pn=128):
            return R[p0:p0 + pn, c * S + t0: c * S + t0 + n]

        def kn(c, t0, n, p0=0, pn=128):
            return R[p0:p0 + pn, 8192 + c * S + t0: 8192 + c * S + t0 + n]

        def vv(tb, c0, n):
            return R[:, 16384 + tb * 512 + c0: 16384 + tb * 512 + c0 + n]

        def c0v(c, col, n):
            return R[:, 24576 + c * CST + col: 24576 + c * CST + col + n]

        def uv(j, t0, n):
            return R[:, j * 1024 + t0: j * 1024 + t0 + n]

        def pcol(l, i):
            return par[l % 2][:, i:i + 1]

        def MMG(bank, pairs, reads, first=True, last=True, out=None):
            n = len(pairs)
            if out is None:
                out = ps[bank][:]

            def fn(t):
                r = None
                for i, (l, rr) in enumerate(pairs):
                    r = t.matmul(out, l, rr, start=(first and i == 0), stop=(last and i == n - 1))
                return r
            return p.op("pe", fn, reads=list(reads), writes=[PB[bank]])

        def ACT(out, in_, func, reads, writes, bias=None, scale=None):
            kw = {}
            if bias is not None:
                kw["bias"] = bias
            if scale is not None:
                kw["scale"] = scale
            return p.op("act", lambda s: s.activation(out=out, in_=in_, func=func, **kw), reads=reads, writes=writes)

        def TT(eng, out, in0, in1, op, reads, writes):
            return p.op(eng, lambda v: v.tensor_tensor(out=out, in0=in0, in1=in1, op=op), reads=reads, writes=writes)

        def TS(eng, out, in0, s1, op0, reads, writes, s2=None, op1=None):
            if op1 is None:
                return p.op(eng, lambda v: v.tensor_scalar(out=out, in0=in0, scalar1=s1, scalar2=None, op0=op0), reads=reads, writes=writes)
            return p.op(eng, lambda v: v.tensor_scalar(out=out, in0=in0, scalar1=s1, scalar2=s2, op0=op0, op1=op1), reads=reads, writes=writes)

        def STT(eng, out, in0, scalar, in1, op0, op1, reads, writes):
            return p.op(eng, lambda v: v.scalar_tensor_tensor(out=out, in0=in0, scalar=scalar, in1=in1, op0=op0, op1=op1), reads=reads, writes=writes)

        def COPY(eng, out, in_, reads, writes):
            if eng == "act":
                return ACT(out, in_, AF.Copy, reads, writes)
            return p.op(eng, lambda v: v.tensor_copy(out=out, in_=in_), reads=reads, writes=writes)

        p.op("pool", lambda g: g.memset(ones[:], 1.0), writes=[CONSTB])
        p.op("pool", lambda g: g.memset(bdm[:], 0.0), writes=[CONSTB])
        p.op("pool", lambda g: g.memset(bdm[0:64, 0:64], 1.0), writes=[CONSTB])
        p.op("pool", lambda g: g.memset(bdm[64:128, 64:128], 1.0), writes=[CONSTB])
        p.op("pool", lambda g: g.memset(tri[:], 1.0), writes=[CONSTB])
        p.op("pool", lambda g: g.affine_select(out=tri[:], in_=tri[:], pattern=[[-1, 128]], compare_op=ALU.is_ge,
                                               fill=0.0, base=0, channel_multiplier=1), reads=[CONSTB], writes=[CONSTB])
        p.op("pool", lambda g: g.memset(ident[:], 1.0), writes=[CONSTB])
        p.op("pool", lambda g: g.affine_select(out=ident[:], in_=ident[:], pattern=[[-1, 128]], compare_op=ALU.is_equal,
                                               fill=0.0, base=0, channel_multiplier=1), reads=[CONSTB], writes=[CONSTB])
        p.op("pool", lambda g: g.memset(maskw[:], 0.0), writes=[CONSTB])
        p.op("pool", lambda g: g.affine_select(out=maskw[:], in_=maskw[:], pattern=[[1, 896]], compare_op=ALU.is_gt,
                                               fill=-1024.0, base=-384, channel_multiplier=-1), reads=[CONSTB], writes=[CONSTB])
        for c in range(4):
            p.op("pool", lambda g, c=c: g.memset(c0v(c, 0, 32), 0.0), writes=[CONSTB])
        p.barrier()
        CONSTB.const = True

        groups = []
        for l in range(n_layers):
            for g in range(5):
                groups.append((win_d[l * 5 + g], 4096))
            for g in range(2):
                groups.append((wout_d[l * 2 + g], 4096))
            for hf in range(2):
                for g in range(11):
                    groups.append((wup_d[l * 11 + g], 4096))
                for g in range(8):
                    groups.append((wdn_d[l * 8 + g], 2816))
        wst = {"issued": 0, "done": 0, "cur": 0}

        def w_issue():
            while wst["issued"] < len(groups) and wst["issued"] - NS < wst["done"]:
                n = wst["issued"]
                src, ncol = groups[n]
                slot = n % NS
                p.op("pool", lambda g, src=src, ncol=ncol, slot=slot: g.dma_start(out=wsl[slot][:, 0:ncol], in_=src),
                     writes=[WB[slot]], dma_sem="w%d" % slot)
                wst["issued"] += 1

        def w_get():
            n = wst["cur"]
            assert n < wst["issued"], "weight group not issued"
            wst["cur"] += 1
            return n % NS

        def w_done():
            wst["done"] += 1
            w_issue()

        def tt_view(ap2, tt):
            return bass.AP(ap2.tensor, ap2.offset + tt * 512, [ap2.ap[0], [S, 8], [1, 512]])
        xT_all = xT[:]
        p.op("sp", lambda g: g.dma_start(out=par[0][:], in_=par_d[0]), writes=[PARB[0]], dma_sem="par0")
        for tt in range(4):
            p.op("sp", lambda g, tt=tt: g.dma_start(out=tt_view(xT_all, tt), in_=tt_view(xin, tt)),
                 writes=[XB[c][tt] for c in range(8)], dma_sem="x%d" % tt)
        w_issue()
        outs = []

        bank_all = Rot(range(8))

        def rmsnorm(l, tts, gcol0):
            for tt in tts:
                bank = bank_all.next()
                for c in range(8):
                    i = c % 2
                    ACT(sbf[i][:], xs(c, tt), AF.Square, [XB[c][tt]], [SBB[i]])
                    MMG(bank, [(ones[:], sbf[i][:])], [SBB[i], CONSTB], first=(c == 0), last=(c == 7))
                a = tt % 2
                ACT(sf[a][:, 0:512], ps[bank][:], AF.Ln, [PB[bank]], [SFB[a]], bias=EPS, scale=1.0 / D)
                ACT(sf[a][:, 0:512], sf[a][:, 0:512], AF.Exp, [SFB[a]], [SFB[a]], scale=-0.5)
                for c in range(8):
                    STT("dve", hs(c, tt * 512, 512), xs(c, tt), pcol(l, gcol0 + c), sf[a][:, 0:512], ALU.mult, ALU.mult,
                        [XB[c][tt], SFB[a], PARB[l % 2]], [HB[c][tt]])

        for l in range(n_layers):
            pl = l % 2
            if l + 1 < n_layers:
                p.op("sp", lambda g, l=l: g.dma_start(out=par[(l + 1) % 2][:], in_=par_d[l + 1]), writes=[PARB[(l + 1) % 2]],
                     dma_sem="par%d" % ((l + 1) % 2))
            TS("pool", gqs[pl][:], pcol(l, 8), 0.125, ALU.mult, [PARB[pl]], [GQB[pl]])

            if stop_after == 'A0':
                break
            rmsnorm(l, range(4) if l == 0 else [2, 3], 0)
            p.barrier()

            if stop_after == 'A1':
                break
            for g in range(2):
                slot = w_get()
                W = wsl[slot]
                items = [(oc, tt) for oc in range(4) for tt in range(4)]
                banks_qk = {}

                def qk_main(idx, W=W, slot=slot):
                    oc, tt = items[idx]
                    bank = bank_all.next()
                    banks_qk[idx] = bank
                    MMG(bank, [(W[:, c * 512 + oc * 128: c * 512 + oc * 128 + 128], hs(c, tt * 512, 512)) for c in range(8)],
                        [WB[slot]] + [HB[c][tt] for c in range(8)])
                    i = idx % 3
                    COPY("dve", sf6[2 * i][:, 0:512], ps[bank][:], [PB[bank]], [SFB6[2 * i]])
                    ACT(sbf3[i], sf6[2 * i][:, 0:512], AF.Square, [SFB6[2 * i]], [SBB[i]])

                def qk_tail(idx, g=g):
                    oc, tt = items[idx]
                    i = idx % 3
                    bank2 = bank_all.next()
                    MMG(bank2, [(bdm[:], sbf3[i])], [SBB[i], CONSTB])
                    ACT(sf6[2 * i + 1][:, 0:512], ps[bank2][:], AF.Ln, [PB[bank2]], [SFB6[2 * i + 1]], bias=EPS, scale=1.0 / 64)
                    ACT(sf6[2 * i + 1][:, 0:512], sf6[2 * i + 1][:, 0:512], AF.Exp, [SFB6[2 * i + 1]], [SFB6[2 * i + 1]], scale=-0.5)

                def qk_fin(idx, g=g):
                    oc, tt = items[idx]
                    i = idx % 3
                    if g == 0:
                        STT("dve", qn(oc, tt * 512, 512), sf6[2 * i][:, 0:512], gqs[pl][:], sf6[2 * i + 1][:, 0:512], ALU.mult, ALU.mult,
                            [SFB6[2 * i], SFB6[2 * i + 1], GQB[pl]], [QB[oc][tt]])
                    else:
                        STT("dve", kn(oc, tt * 512, 512), sf6[2 * i][:, 0:512], pcol(l, 9), sf6[2 * i + 1][:, 0:512], ALU.mult, ALU.mult,
                            [SFB6[2 * i], SFB6[2 * i + 1], PARB[pl]], [KB[oc][tt]])

                NI = len(items)
                for idx in range(NI + 3):
                    if 0 <= idx - 3 < NI:
                        qk_fin(idx - 3)
                    if 0 <= idx - 2 < NI:
                        qk_tail(idx - 2)
                    if idx < NI:
                        qk_main(idx)
                w_done()
            slot = w_get()
            W = wsl[slot]
            for tb in range(16):
                bank = bank_all.next()
                MMG(bank, [(hs(c, tb * 128, 128), W[:, c * 512: c * 512 + 512]) for c in range(8)],
                    [WB[slot]] + [HB[c][tb // 4] for c in range(8)])
                COPY("act" if tb % 2 == 0 else "dve", vv(tb, 0, 512), ps[bank][:], [PB[bank]], [VB[tb]])
            w_done()
            for gg in range(2):
                slot = w_get()
                W = wsl[slot]
                for o2 in range(2):
                    oc = 2 * gg + o2
                    for tt in range(4):
                        ba = bank_all.next()
                        bb = bank_all.next()
                        rd = [WB[slot]] + [HB[c][tt] for c in range(8)]
                        MMG(ba, [(W[:, c * 512 + o2 * 128: c * 512 + o2 * 128 + 128], hs(c, tt * 512, 512)) for c in range(8)], rd)
                        MMG(bb, [(W[:, c * 512 + 256 + o2 * 128: c * 512 + 256 + o2 * 128 + 128], hs(c, tt * 512, 512)) for c in range(8)], rd)
                        i = (oc * 4 + tt) % 2
                        ACT(sf[i][:, 0:512], ps[bb][:], AF.Sigmoid, [PB[bb]], [SFB[i]])
                        TT("dve", c0v(oc, 32 + tt * 512, 512), ps[ba][:], sf[i][:, 0:512], ALU.mult, [PB[ba], SFB[i]], [C0B[oc][tt]])
                w_done()

            if stop_after == 'A2':
                break
            idv = ident[:]

            def dbuild(c, k0, k1, buf):
                wv = par[pl][:, 10 + c * 31 + k0: 10 + c * 31 + k1]
                dgv = diag[:, k0 * 128:k1 * 128]
                d_in0 = bass.AP(idv.tensor, idv.offset, [idv.ap[0], [0, k1 - k0], idv.ap[1]])
                d_in1 = bass.AP(wv.tensor, wv.offset, [wv.ap[0], wv.ap[1], [0, 128]])
                d_out = bass.AP(dgv.tensor, dgv.offset, [dgv.ap[0], [128, k1 - k0], [1, 128]])
                p.op("pool", lambda v, a=d_out, b=d_in0, cc=d_in1: v.tensor_tensor(out=a, in0=b, in1=cc, op=ALU.mult),
                     reads=[CONSTB, PARB[pl]], writes=[buf])

            def conv_chunk(half, c):
                tts = [2 * half, 2 * half + 1]
                banks = {tt: bank_all.next() for tt in tts}
                dbuild(c, 0, 16, DGA)
                for tt in tts:
                    rd = [DGA, C0B[c][tt]] + ([C0B[c][tt - 1]] if tt else [])
                    MMG(banks[tt], [(diag[:, k * 128:(k + 1) * 128], c0v(c, tt * 512 + k + 2, 512)) for k in range(0, 16)], rd, first=True, last=False)
                dbuild(c, 16, 31, DGBB)
                for tt in tts:
                    rd = [DGBB, C0B[c][tt]] + ([C0B[c][tt - 1]] if tt else [])
                    MMG(banks[tt], [(diag[:, k * 128:(k + 1) * 128], c0v(c, tt * 512 + k + 2, 512)) for k in range(16, 31)], rd, first=False, last=True)
                for tt in tts:
                    ACT(hs(4 + c, tt * 512, 512), ps[banks[tt]][:], AF.Identity, [PB[banks[tt]], PARB[pl]], [HB[4 + c][tt]], bias=pcol(l, 134 + c))

            def ln_steps(half):
                steps = []
                bms, bss, MR = {}, {}, {}

                def yv(c, tt):
                    return hs(4 + c, tt * 512, 512)

                def st_a(t2):
                    tt = 2 * half + t2
                    bms[t2] = bank_all.next()
                    bss[t2] = bank_all.next()
                    for c in range(4):
                        i = c % 2
                        ACT(sbf[i][:], yv(c, tt), AF.Square, [HB[4 + c][tt]], [SBB[i]])
                        MMG(bms[t2], [(ones[:], yv(c, tt))], [HB[4 + c][tt], CONSTB], first=(c == 0), last=(c == 3))
                        MMG(bss[t2], [(ones[:], sbf[i][:])], [SBB[i], CONSTB], first=(c == 0), last=(c == 3))
                steps.append(lambda: st_a(0))
                steps.append(lambda: st_a(1))

                def st_b():
                    for t2 in range(2):
                        mi, ri = (0, 1) if t2 == 0 else (4, 5)
                        MR[t2] = (sf6[mi][:, 0:512], sf6[ri][:, 0:512], mi, ri)
                        ACT(MR[t2][0], ps[bms[t2]][:], AF.Copy, [PB[bms[t2]]], [SFB6[mi]], scale=1.0 / 512)
                    for t2 in range(2):
                        M_, R_, mi, ri = MR[t2]
                        TT("dve", R_, M_, M_, ALU.mult, [SFB6[mi]], [SFB6[ri]])
                        STT("dve", R_, ps[bss[t2]][:], 1.0 / 512, R_, ALU.mult, ALU.subtract, [PB[bss[t2]], SFB6[ri]], [SFB6[ri]])
                    for t2 in range(2):
                        M_, R_, mi, ri = MR[t2]
                        ACT(R_, R_, AF.Ln, [SFB6[ri]], [SFB6[ri]], bias=EPS)
                    for t2 in range(2):
                        M_, R_, mi, ri = MR[t2]
                        ACT(R_, R_, AF.Exp, [SFB6[ri]], [SFB6[ri]], scale=-0.5)
                steps.append(st_b)

                def st_c(c, t2, n_it):
                    tt = 2 * half + t2
                    M_, R_, mi, ri = MR[t2]
                    i = 2 + n_it % 2
                    TT("dve", sf[i][:, 0:512], yv(c, tt), M_, ALU.subtract, [HB[4 + c][tt], SFB6[mi]], [SFB[i]])
                    TT("dve" if n_it % 2 == 0 else "pool", sf[i][:, 0:512], sf[i][:, 0:512], R_, ALU.mult, [SFB[i], SFB6[ri]], [SFB[i]])
                    ACT(yv(c, tt), sf[i][:, 0:512], AF.Silu, [SFB[i], PARB[pl]], [HB[4 + c][tt]],
                        bias=pcol(l, 142 + c), scale=pcol(l, 138 + c))
                n_it = 0
                for c in range(4):
                    for t2 in range(2):
                        steps.append(lambda c=c, t2=t2, n_it=n_it: st_c(c, t2, n_it))
                        n_it += 1
                return steps

            for c in range(4):
                conv_chunk(0, c)
            pend = ln_steps(0)
            per = [3, 3, 3, 2]
            for c in range(4):
                conv_chunk(1, c)
                for _ in range(per[c]):
                    pend.pop(0)()
            assert not pend
            for st in ln_steps(1):
                st()
            if stop_after == 'A3':
                break
            p.barrier()
            pairs = []
            seqi = 0
            for c in range(4):
                for qt in range(4):
                    kbs = list(range(4 * qt + 3, -1, -1))
                    for n, kb in enumerate(kbs):
                        pairs.append(dict(c=c, qt=qt, kb=kb, first=(n == 0), last=(n == len(kbs) - 1), ob=6 + seqi % 2))
                    seqi += 1
            NU = len(pairs)

            def pv(ap2, a):
                return bass.AP(ap2.tensor, ap2.offset + a, [ap2.ap[0], [512, 2], [1, 512 - a]])

            def acol(i):
                u = pairs[i]
                j = u["kb"] - 4 * u["qt"]
                return 128 * j if j > 0 else 0

            def P1(i):
                u = pairs[i]
                c, qt, kb = u["c"], u["qt"], u["kb"]
                zb = 0
                j = kb - 4 * qt
                dg = j >= 0
                a = acol(i)
                n = 512 - a
                for hh in range(2):
                    MMG(zb + hh, [(kn(c, kb * 128, 128, hh * 64, 64), qn(c, qt * 512 + a, n, hh * 64, 64))], [KB[c][kb // 4], QB[c][qt]],
                        last=not dg, out=ps[zb + hh][:, a:512])
                if dg:
                    mw = maskw[:, 384 - 128 * j + a: 384 - 128 * j + 512]
                    for hh in range(2):
                        MMG(zb + hh, [(ident[:], mw)], [CONSTB], first=False, out=ps[zb + hh][:, a:512])
                ACT(pv(E[i % 3][:, 0:1024], a), pv(PSALL[:, zb * 512: zb * 512 + 1024], a), AF.Exp, [PB[zb], PB[zb + 1]], EBs[i % 3])

            def P1b(i):
                u = pairs[i]
                a = acol(i)
                ACT(pv(SP[i % 2], a), pv(E[i % 3][:, 0:1024], a), AF.Ln, EBs[i % 3], SPB[i % 2], bias=1.0)
                if not u["last"]:
                    if u["first"]:
                        COPY("pool", pv(SS[i % 2], a), pv(SP[i % 2], a), SPB[i % 2], [SSB[i % 2]])
                    else:
                        ap_ = acol(i - 1)
                        TT("pool", pv(SS[i % 2], ap_), pv(SS[(i - 1) % 2], ap_), pv(SP[i % 2], ap_), ALU.add, [SSB[(i - 1) % 2]] + SPB[i % 2], [SSB[i % 2]])
                        if ap_ > a:
                            def v2(ap2, lo, hi):
                                return bass.AP(ap2.tensor, ap2.offset + lo, [ap2.ap[0], [512, 2], [1, hi - lo]])
                            COPY("pool", v2(SS[i % 2], a, ap_), v2(SP[i % 2], a, ap_), SPB[i % 2], [SSB[i % 2]])

            def P2(i):
                u = pairs[i]
                cb = 2 + 2 * (i % 2)
                a = acol(i)
                for hh in range(2):
                    spv = SP[i % 2][:, hh * 512 + a:(hh + 1) * 512]
                    if u["first"]:
                        MMG(cb + hh, [(tri[:], spv)], SPB[i % 2] + [CONSTB], out=ps[cb + hh][:, a:512])
                    else:
                        ap_ = acol(i - 1)
                        MMG(cb + hh, [(tri[:], spv)], SPB[i % 2] + [CONSTB], last=False, out=ps[cb + hh][:, a:512])
                        MMG(cb + hh, [(ones[:], SS[(i - 1) % 2][:, hh * 512 + ap_:(hh + 1) * 512])], [SSB[(i - 1) % 2], CONSTB],
                            first=False, out=ps[cb + hh][:, ap_:512])

            def P2r(i):
                cb = 2 + 2 * (i % 2)
                a = acol(i)
                ACT(pv(WW[i % 2], a), pv(PSALL[:, cb * 512: cb * 512 + 1024], a), AF.Exp, [PB[cb], PB[cb + 1]], [WWB[i % 2]], scale=-1.0)

            def P2b(i):
                a = acol(i)
                TT("dve", pv(WW[i % 2], a), pv(E[i % 3][:, 0:1024], a), pv(WW[i % 2], a), ALU.mult, EBs[i % 3] + [WWB[i % 2]], [WWB[i % 2]])

            def P3(i):
                u = pairs[i]
                c, qt, kb = u["c"], u["qt"], u["kb"]
                ob = u["ob"]
                a = acol(i)

                def fn(t, ob=ob, kb=kb, c=c, i=i, a=a, first=u["first"], last=u["last"]):
                    t.matmul(ps[ob][0:64, a:512], vv(kb, c * 128, 64), WW[i % 2][:, a:512], start=first, stop=last, skip_group_check=True)
                    return t.matmul(ps[ob][64:128, a:512], vv(kb, c * 128 + 64, 64), WW[i % 2][:, 512 + a:1024], start=first, stop=last, skip_group_check=True)
                p.op("pe", fn, reads=[VB[kb], WWB[i % 2]], writes=[PB[ob]])
                if u["last"]:
                    COPY("dve", hs(c, qt * 512, 512), ps[ob], [PB[ob]], [HB[c][qt]])

            for it in range(NU + 3):
                if 0 <= it - 2 < NU:
                    P2r(it - 2)
                if it < NU:
                    P1(it)
                if 0 <= it - 1 < NU:
                    P2(it - 1)
                if it < NU:
                    P1b(it)
                if 0 <= it - 2 < NU:
                    P2b(it - 2)
                if 0 <= it - 3 < NU:
                    P3(it - 3)
            p.barrier()

            if stop_after == 'A4':
                break
            for g in range(2):
                slot = w_get()
                W = wsl[slot]
                for o4 in range(4):
                    oc = g * 4 + o4
                    for tt in range(4):
                        bank = bank_all.next()
                        MMG(bank, [(W[:, c * 512 + o4 * 128: c * 512 + o4 * 128 + 128], hs(c, tt * 512, 512)) for c in range(8)],
                            [WB[slot]] + [HB[c][tt] for c in range(8)])
                        TT("dve", xs(oc, tt), xs(oc, tt), ps[bank][:], ALU.add, [XB[oc][tt], PB[bank]], [XB[oc][tt]])
                w_done()

            if stop_after == 'A5':
                break
            p.op("pool", lambda g: g.memset(halo[:], 0.0), writes=[HALOB])
            rmsnorm(l, range(4), 146)
            for hf in range(2):
                pend_u = []
                pend_s = []
                for pg in range(11):
                    slot = w_get()
                    W = wsl[slot]
                    for jj in range(2):
                        j = 2 * pg + jj
                        for t2 in range(2):
                            tt = 2 * hf + t2
                            bg = bank_all.next()
                            bv = bank_all.next()
                            rd = [WB[slot]] + [HB[c][tt] for c in range(8)]
                            MMG(bg, [(W[:, c * 512 + jj * 128: c * 512 + jj * 128 + 128], hs(c, tt * 512, 512)) for c in range(8)], rd)
                            MMG(bv, [(W[:, c * 512 + 256 + jj * 128: c * 512 + 256 + jj * 128 + 128], hs(c, tt * 512, 512)) for c in range(8)], rd)
                            i = (j * 2 + t2) % 2
                            gb, t1 = sf[i], sf[2 + i]
                            COPY("pool", gb[:, 0:2], halo[:, 2 * j:2 * j + 2], [HALOB], [SFB[i]])
                            ACT(gb[:, 2:514], ps[bg][:], AF.Copy, [PB[bg]], [SFB[i]])
                            COPY("pool", halo[:, 2 * j:2 * j + 2], gb[:, 512:514], [SFB[i]], [HALOB])
                            if len(pend_u) > 1:
                                pend_u.pop(0)()
                            wc = 154 + j * 3
                            TS("dve", t1[:, 0:512], gb[:, 2:514], pcol(l, wc + 2), ALU.mult, [SFB[i], PARB[pl]], [SFB[2 + i]])
                            STT("dve", t1[:, 0:512], gb[:, 1:513], pcol(l, wc + 1), t1[:, 0:512], ALU.mult, ALU.add, [SFB[i], SFB[2 + i], PARB[pl]], [SFB[2 + i]])
                            STT("dve", t1[:, 0:512], gb[:, 0:512], pcol(l, wc + 0), t1[:, 0:512], ALU.mult, ALU.add, [SFB[i], SFB[2 + i], PARB[pl]], [SFB[2 + i]])
                            if pend_s:
                                pend_s.pop(0)()
                            pend_s.append(lambda j=j, t1=t1, i=i: ACT(t1[:, 0:512], t1[:, 0:512], AF.Silu, [SFB[2 + i], PARB[pl]], [SFB[2 + i]], bias=pcol(l, 220 + j)))
                            pend_u.append(lambda j=j, t2=t2, t1=t1, bv=bv, i=i: TT("dve", uv(j, t2 * 512, 512), t1[:, 0:512], ps[bv][:], ALU.mult,
                                                                                  [SFB[2 + i], PB[bv]], [UB[j][t2]]))
                    w_done()
                while pend_s:
                    pend_s.pop(0)()
                while pend_u:
                    pend_u.pop(0)()
                for oc in range(8):
                    slot = w_get()
                    W = wsl[slot]
                    for t2 in range(2):
                        tt = 2 * hf + t2
                        bank = bank_all.next()
                        MMG(bank, [(W[:, j * 128:(j + 1) * 128], uv(j, t2 * 512, 512)) for j in range(NJ)],
                            [WB[slot]] + [UB[j][t2] for j in range(NJ)])
                        TT("dve", xs(oc, tt), xs(oc, tt), ps[bank][:], ALU.add, [XB[oc][tt], PB[bank]], [XB[oc][tt]])
                    w_done()
                if l + 1 < n_layers:
                    if hf == 0:
                        rmsnorm(l + 1, [0, 1], 0)
                elif not dbg:
                    for t2 in range(2):
                        tt = 2 * hf + t2
                        outs.append(p.op("sp", lambda g, tt=tt: g.dma_start(out=tt_view(yout, tt), in_=tt_view(xT_all, tt)),
                                         reads=[XB[c][tt] for c in range(8)], dma_sem="out"))

        if dbg:
            p.barrier()
            p.op("sp", lambda g: g.dma_start(out=dbg_h, in_=hT[:]), dma_sem="out")
            p.op("sp", lambda g: g.dma_start(out=dbg_r, in_=R[:]), dma_sem="out")
        if not outs:
            for tt in range(4):
                outs.append(p.op("sp", lambda g, tt=tt: g.dma_start(out=tt_view(yout, tt), in_=tt_view(xT_all, tt)),
                                 reads=[XB[c][tt] for c in range(8)], dma_sem="out"))
        p.emit(final_waits=[outs[-1]])
        build_program.last_stats = {q: (len(s), sum(len(o.waits) for o in s)) for q, s in p.streams.items()}
    return nc


def _pm(w, kc):
    n = w.shape[1]
    return np.ascontiguousarray(w.reshape(kc, 128, n).transpose(1, 0, 2).reshape(128, kc * n))


def prep_weights(inp, n_layers=DEPTH):
    f = lambda a: np.asarray(a, dtype=np.float32)
    w_in, w_out, w_up, w_down = f(inp["w_in"]), f(inp["w_out"]), f(inp["w_up"]), f(inp["w_down"])
    win = []
    for l in range(n_layers):
        for g in range(3):
            win.append(_pm(w_in[l][:, g * 512:(g + 1) * 512], 8))
        for gg in range(2):
            cols = np.concatenate([w_in[l][:, 1536 + gg * 256: 1536 + (gg + 1) * 256], w_in[l][:, 2048 + gg * 256: 2048 + (gg + 1) * 256]], axis=1)
            win.append(_pm(cols, 8))
    win = np.stack(win)
    wout = np.stack([_pm(w_out[l][:, g * 512:(g + 1) * 512], 8) for l in range(n_layers) for g in range(2)])
    wup = []
    for l in range(n_layers):
        for pg in range(11):
            cols = np.concatenate([w_up[l][:, pg * 256:(pg + 1) * 256], w_up[l][:, DFF + pg * 256: DFF + (pg + 1) * 256]], axis=1)
            wup.append(_pm(cols, 8))
    wup = np.stack(wup)
    wdn = np.stack([_pm(w_down[l][:, oc * 128:(oc + 1) * 128], NJ) for l in range(n_layers) for oc in range(8)])
    par = np.zeros((n_layers, 128, NPAR), np.float32)
    for l in range(n_layers):
        P = par[l]
        P[:, 0:8] = f(inp["norm1_g"])[l].reshape(8, 128).T
        P[:, 8] = np.tile(f(inp["q_norm_g"])[l], 2)
        P[:, 9] = np.tile(f(inp["k_norm_g"])[l], 2)
        cw = f(inp["conv_dw_w"])[l]
        for c in range(4):
            P[:, 10 + c * 31: 10 + (c + 1) * 31] = cw[:, c * 128:(c + 1) * 128].T
        P[:, 134:138] = f(inp["conv_dw_b"])[l].reshape(4, 128).T
        P[:, 138:142] = f(inp["conv_ln_g"])[l].reshape(4, 128).T
        P[:, 142:146] = f(inp["conv_ln_b"])[l].reshape(4, 128).T
        P[:, 146:154] = f(inp["norm2_g"])[l].reshape(8, 128).T
        fw = f(inp["ffn_dw_w"])[l]
        for j in range(NJ):
            P[:, 154 + j * 3: 154 + j * 3 + 3] = fw[:, j * 128:(j + 1) * 128].T
        P[:, 220:242] = f(inp["ffn_dw_b"])[l].reshape(NJ, 128).T
    return dict(par=par, win=win, wout=wout, wup=wup, wdn=wdn)


def prep_x(xb):
    return np.ascontiguousarray(np.asarray(xb, np.float32).T.reshape(8, 128, S).transpose(1, 0, 2).reshape(128, 8 * S))


def unprep_y(y):
    return np.ascontiguousarray(y.reshape(128, 8, S).transpose(1, 0, 2).reshape(D, S).T)


_NC_CACHE = {}


def kernel(**inputs):
    x = np.asarray(inputs["x"], dtype=np.float32)
    B = x.shape[0]
    if DEPTH not in _NC_CACHE:
        _NC_CACHE[DEPTH] = build_program(DEPTH)
    nc = _NC_CACHE[DEPTH]
    wts = prep_weights(inputs, DEPTH)
    in_maps = []
    for b in range(B):
        m = dict(wts)
        m["xin"] = prep_x(x[b])
        in_maps.append(m)
    res = run_bass_kernel_spmd(nc, in_maps, core_ids=list(range(B)))
    out = np.stack([unprep_y(np.asarray(res.results[b]["yout"], dtype=np.float32)) for b in range(B)])
    return out.astype(np.float32)
```
